# Optimizing a Trainium2 kernel written in Bass

```python
import jax, jax.numpy as jnp
from jax import lax
import numpy as np

D_MODEL = 4096
BATCH = 1
SEQ = 8192
DEPTH = 2

GRID_W = 64
CTX_LEN = 256
D_POOL = D_MODEL // 4
POOL_WINDOWS = (2, 4, 8, 16)
N_POOL_GROUPS = 4
POOL_GC = D_POOL // N_POOL_GROUPS
HEAD_DIM = 128
D_ATTN = D_MODEL // 2
N_HEADS = D_ATTN // HEAD_DIM
WIN_ROWS = 8
WIN_COLS = 16
D_FNET = D_MODEL // 4
N_FNET_GROUPS = 4
FNET_GC = D_FNET // N_FNET_GROUPS
D_IN = D_POOL + 3 * D_ATTN + D_FNET
N_BRANCHES = 3
D_FF = 11008
CONV_W = 3
N_MOD = 6
EPS = 1e-6

kernel_name = 'hybrid_pool_natten_fnet_dit_block'


def rms_norm(x, w):
    x32 = x.astype(jnp.float32)
    y = x32 * lax.rsqrt(jnp.mean(x32 * x32, axis=-1, keepdims=True) + EPS)
    return y.astype(x.dtype) * w


def modulate(h, shift, scale):
    return h * (1 + scale) + shift


def to_heads(t):
    return t.reshape(t.shape[:-1] + (N_HEADS, HEAD_DIM))


def pool_mix(u, pool_w, pool_scale):
    B, S, _ = u.shape
    ug = u.reshape(B, S, N_POOL_GROUPS, POOL_GC)
    cs = jnp.pad(jnp.cumsum(ug.astype(jnp.float32), axis=1), ((0, 0), (1, 0), (0, 0), (0, 0)))
    t = jnp.arange(S)
    means = []
    for g, w in enumerate(POOL_WINDOWS):
        lo = jnp.clip(t - w // 2, 0, S - 1)
        hi = jnp.clip(t + (w - 1 - w // 2), 0, S - 1)
        cnt = (hi - lo + 1).astype(jnp.float32)[None, :, None]
        means.append((cs[:, hi + 1, g] - cs[:, lo, g]) / cnt)
    pooled = jnp.stack(means, axis=2)
    d = (pooled - ug.astype(jnp.float32)).astype(u.dtype)
    y = jnp.einsum('bsgc,gce->bsge', d, pool_w).reshape(B, S, D_POOL)
    return y * pool_scale


def fourier_mix(u, fnet_w):
    B, S, _ = u.shape
    ug = u.reshape(B, S, N_FNET_GROUPS, FNET_GC).astype(jnp.float32)
    f = jnp.fft.fft2(ug, axes=(1, 3), norm='ortho').real.astype(u.dtype)
    return jnp.einsum('bsgc,gce->bsge', f, fnet_w).reshape(B, S, D_FNET)


def neighborhood_attention(q, k, v, kc, vc, rpb):
    B, S, H, hd = q.shape
    rows = S // GRID_W
    kr = min(WIN_ROWS, rows)
    scale = HEAD_DIM ** -0.5
    qg = q.reshape(B, rows, GRID_W, H, hd)
    kg = k.reshape(B, rows, GRID_W, H, hd)
    vg = v.reshape(B, rows, GRID_W, H, hd)
    col = jnp.arange(GRID_W)
    col_start = jnp.clip(col - WIN_COLS // 2, 0, GRID_W - WIN_COLS)
    col_idx = col_start[:, None] + jnp.arange(WIN_COLS)[None, :]
    col_off = col_idx - col[:, None] + (WIN_COLS - 1)
    bias_cols = rpb[:, :, col_off]

    def row_step(args):
        r, q_row = args
        r_start = jnp.clip(r - kr // 2, 0, rows - kr)
        k_rows = lax.dynamic_slice_in_dim(kg, r_start, kr, axis=1)
        v_rows = lax.dynamic_slice_in_dim(vg, r_start, kr, axis=1)
        k_win = k_rows[:, :, col_idx]
        v_win = v_rows[:, :, col_idx]
        row_off = r_start + jnp.arange(kr) - r + (WIN_ROWS - 1)
        bias = bias_cols[:, row_off].transpose(0, 2, 1, 3)
        s_win = jnp.einsum('bwhd,brwchd->bhwrc', q_row, k_win) * scale + bias[None]
        s_ctx = jnp.einsum('bwhd,bnhd->bhwn', q_row, kc) * scale
        s = jnp.concatenate([s_win.reshape(B, H, GRID_W, kr * WIN_COLS), s_ctx], axis=-1)
        p = jax.nn.softmax(s.astype(jnp.float32), axis=-1).astype(v.dtype)
        p_win = p[..., :kr * WIN_COLS].reshape(B, H, GRID_W, kr, WIN_COLS)
        p_ctx = p[..., kr * WIN_COLS:]
        return (jnp.einsum('bhwrc,brwchd->bwhd', p_win, v_win)
                + jnp.einsum('bhwn,bnhd->bwhd', p_ctx, vc))

    out = lax.map(row_step, (jnp.arange(rows), qg.transpose(1, 0, 2, 3, 4)))
    return out.transpose(1, 0, 2, 3, 4).reshape(B, S, H * hd)


def context_attention(qc, kc, vc):
    B, N, H, hd = qc.shape
    s = jnp.einsum('bnhd,bmhd->bhnm', qc, kc) * (HEAD_DIM ** -0.5)
    p = jax.nn.softmax(s.astype(jnp.float32), axis=-1).astype(vc.dtype)
    return jnp.einsum('bhnm,bmhd->bnhd', p, vc).reshape(B, N, H * hd)


def merge_branches(h, y_pool, y_attn, y_fnet, w_gate, b_gate, w_br_pool, w_br_attn, w_br_fnet, w_o):
    g_pool, g_attn, g_fnet = jnp.split(jax.nn.sigmoid(h @ w_gate + b_gate), N_BRANCHES, axis=-1)
    m = g_pool * (y_pool @ w_br_pool) + g_attn * (y_attn @ w_br_attn) + g_fnet * (y_fnet @ w_br_fnet)
    return m @ w_o


def conv_ffn(h, w_up, conv_w, conv_b, w_down):
    u = h @ w_up
    up = jnp.pad(u, ((0, 0), (1, 1), (0, 0)))
    u = up[:, :-2] * conv_w[0] + up[:, 1:-1] * conv_w[1] + up[:, 2:] * conv_w[2] + conv_b
    gate, val = jnp.split(u, 2, axis=-1)
    return (jax.nn.silu(gate) * val) @ w_down


def setup_inputs(seed: int = 0) -> dict:
    key = jax.random.key(seed)
    ks = jax.random.split(key, 26)
    L, D = DEPTH, D_MODEL

    def nrm(k, shape, scale):
        return jax.random.normal(k, shape, jnp.float32) * scale

    return {
        'x': nrm(ks[0], (BATCH, SEQ, D), 1.0),
        'c': nrm(ks[1], (BATCH, D), 1.0),
        'ctx': nrm(ks[2], (BATCH, CTX_LEN, D), 1.0),
        'c_ctx': nrm(ks[3], (D,), 1.0),
        'w_ada': nrm(ks[4], (L, D, N_MOD * D), 0.5 * D ** -0.5),
        'b_ada': nrm(ks[5], (L, N_MOD * D), 0.02),
        'norm1_w': 1.0 + nrm(ks[6], (L, D), 0.02),
        'norm2_w': 1.0 + nrm(ks[7], (L, D), 0.02),
        'w_in': nrm(ks[8], (L, D, D_IN), D ** -0.5),
        'pool_w': nrm(ks[9], (L, N_POOL_GROUPS, POOL_GC, POOL_GC), POOL_GC ** -0.5),
        'pool_scale': 1.0 + nrm(ks[10], (L, D_POOL), 0.02),
        'q_norm_w': 1.0 + nrm(ks[11], (L, HEAD_DIM), 0.02),
        'k_norm_w': 1.0 + nrm(ks[12], (L, HEAD_DIM), 0.02),
        'rpb': nrm(ks[13], (L, N_HEADS, 2 * WIN_ROWS - 1, 2 * WIN_COLS - 1), 0.1),
        'fnet_w': nrm(ks[14], (L, N_FNET_GROUPS, FNET_GC, FNET_GC), FNET_GC ** -0.5),
        'w_gate': nrm(ks[15], (L, D, N_BRANCHES * D), D ** -0.5),
        'b_gate': nrm(ks[16], (L, N_BRANCHES * D), 0.02),
        'w_br_pool': nrm(ks[17], (L, D_POOL, D), D_POOL ** -0.5),
        'w_br_attn': nrm(ks[18], (L, D_ATTN, D), D_ATTN ** -0.5),
        'w_br_fnet': nrm(ks[19], (L, D_FNET, D), D_FNET ** -0.5),
        'w_o': nrm(ks[20], (L, D, D), D ** -0.5),
        'w_up': nrm(ks[21], (L, D, 2 * D_FF), D ** -0.5),
        'conv_w': nrm(ks[22], (L, CONV_W, 2 * D_FF), CONV_W ** -0.5),
        'conv_b': nrm(ks[23], (L, 2 * D_FF), 0.02),
        'w_down': nrm(ks[24], (L, D_FF, D), D_FF ** -0.5),
    }


def reference(x, c, ctx, c_ctx, w_ada, b_ada, norm1_w, norm2_w, w_in, pool_w, pool_scale,
              q_norm_w, k_norm_w, rpb, fnet_w, w_gate, b_gate, w_br_pool, w_br_attn, w_br_fnet,
              w_o, w_up, conv_w, conv_b, w_down):
    split_at = [D_POOL, D_POOL + D_ATTN, D_POOL + 2 * D_ATTN, D_POOL + 3 * D_ATTN]
    k_lo, k_hi = D_POOL + D_ATTN, D_POOL + 3 * D_ATTN
    xc = ctx
    s_lat = jax.nn.silu(c)
    s_ctx = jax.nn.silu(c_ctx)
    for l in range(DEPTH):
        last = l == DEPTH - 1
        mod = (s_lat @ w_ada[l] + b_ada[l])[:, None, :]
        mod_c = s_ctx @ w_ada[l] + b_ada[l]
        sh1, sc1, g1, sh2, sc2, g2 = jnp.split(mod, N_MOD, axis=-1)
        csh1, csc1, cg1, csh2, csc2, cg2 = jnp.split(mod_c, N_MOD, axis=-1)

        h = modulate(rms_norm(x, norm1_w[l]), sh1, sc1)
        hc = modulate(rms_norm(xc, norm1_w[l]), csh1, csc1)
        u_pool, q, k, v, u_fnet = jnp.split(h @ w_in[l], split_at, axis=-1)
        q = rms_norm(to_heads(q), q_norm_w[l])
        k = rms_norm(to_heads(k), k_norm_w[l])
        v = to_heads(v)
        if last:
            kc, vc = jnp.split(hc @ w_in[l][:, k_lo:k_hi], 2, axis=-1)
        else:
            uc_pool, qc, kc, vc, uc_fnet = jnp.split(hc @ w_in[l], split_at, axis=-1)
            qc = rms_norm(to_heads(qc), q_norm_w[l])
        kc = rms_norm(to_heads(kc), k_norm_w[l])
        vc = to_heads(vc)

        y_attn = neighborhood_attention(q, k, v, kc, vc, rpb[l])
        y_pool = pool_mix(u_pool, pool_w[l], pool_scale[l])
        y_fnet = fourier_mix(u_fnet, fnet_w[l])
        x = x + g1 * merge_branches(h, y_pool, y_attn, y_fnet, w_gate[l], b_gate[l],
                                    w_br_pool[l], w_br_attn[l], w_br_fnet[l], w_o[l])

        h2 = modulate(rms_norm(x, norm2_w[l]), sh2, sc2)
        x = x + g2 * conv_ffn(h2, w_up[l], conv_w[l], conv_b[l], w_down[l])

        if not last:
            yc_attn = context_attention(qc, kc, vc)
            yc_pool = pool_mix(uc_pool, pool_w[l], pool_scale[l])
            yc_fnet = fourier_mix(uc_fnet, fnet_w[l])
            xc = xc + cg1 * merge_branches(hc, yc_pool, yc_attn, yc_fnet, w_gate[l], b_gate[l],
                                           w_br_pool[l], w_br_attn[l], w_br_fnet[l], w_o[l])
            hc2 = modulate(rms_norm(xc, norm2_w[l]), csh2, csc2)
            xc = xc + cg2 * conv_ffn(hc2, w_up[l], conv_w[l], conv_b[l], w_down[l])
    return x
```

```python
import contextlib
import numpy as np
import ml_dtypes
import concourse.bass as bass
import concourse.mybir as mybir
from concourse.bass_utils import run_bass_kernel_spmd

F32 = mybir.dt.float32
BF16 = mybir.dt.bfloat16
AF = mybir.ActivationFunctionType
ALU = mybir.AluOpType

NCORES = 8
D = 4096
SEQ = 8192
T = 1024
NCTX = 256
NT = T + NCTX
DEPTH = 2
DFF = 11008
NPAIR = 86
HD = 128
NH = 16
EPS = 1e-6
NEG = -30000.0
PV_N1, PV_N2, PV_BG, PV_PS, PV_CW0, PV_CW1, PV_CW2, PV_CB, PV_QW, PV_KW, PV_N = 0, 32, 64, 160, 168, 340, 512, 684, 856, 857, 858
POOL_WINDOWS = (2, 4, 8, 16)


class Reg:
    __slots__ = ('name', 'writer', 'readers')

    def __init__(self, name=''):
        self.name = name
        self.writer = None
        self.readers = {}


class Sync:
    def __init__(self, nc, stack, ndma_ring=8):
        self.nc = nc
        self.engs = {'pe': nc.tensor, 'act': nc.scalar, 'dve': nc.vector, 'pool': nc.gpsimd, 'sp': nc.sync}
        self.sem = {k: stack.enter_context(nc.semaphore('s_' + k)) for k in self.engs}
        self.tick = {k: 0 for k in self.engs}
        self.ring = {q: [stack.enter_context(nc.semaphore(f'd_{q}{i}')) for i in range(ndma_ring)]
                     for q in ('sp', 'pool')}
        self.ringcnt = {q: 0 for q in self.ring}
        self.known = {k: {} for k in self.engs}

    def _wait(self, eng, dep):
        if dep is None:
            return
        skey, sem, val = dep
        if skey == 'pe' and eng == 'pe':
            return
        kn = self.known[eng]
        if kn.get(skey, 0) >= val:
            return
        self.engs[eng].wait_ge(sem, val)
        kn[skey] = val

    def deps(self, eng, reads, writes):
        for r in reads:
            self._wait(eng, r.writer)
        for w in writes:
            self._wait(eng, w.writer)
            for d in list(w.readers.values()):
                self._wait(eng, d)

    def commit(self, dep, reads, writes):
        for r in reads:
            r.readers[dep[0]] = dep
        for w in writes:
            w.writer = dep
            w.readers = {}

    def op(self, eng, ins_fn, reads=(), writes=()):
        self.deps(eng, reads, writes)
        ins = ins_fn()
        self.tick[eng] += 1
        ins.then_inc(self.sem[eng], 1)
        self.commit((eng, self.sem[eng], self.tick[eng]), reads, writes)
        return ins

    def op_nosig(self, eng, ins_fn, reads=(), writes=()):
        self.deps(eng, reads, writes)
        return ins_fn()

    def dma(self, q, out, in_, reads=(), writes=(), **kw):
        ring = self.ring[q]
        n = self.ringcnt[q]
        slot = n % len(ring)
        rnd = n // len(ring)
        sem = ring[slot]
        skey = f'd_{q}{slot}'
        if rnd > 0:
            self._wait(q, (skey, sem, 16 * rnd))
        self.deps(q, reads, writes)
        ins = self.engs[q].dma_start(out=out, in_=in_, **kw)
        ins.then_inc(sem, 16)
        self.ringcnt[q] = n + 1
        dep = (skey, sem, 16 * (rnd + 1))
        self.commit(dep, reads, writes)
        return dep

    def allgather(self, in_ap, out_ap, rin, rout):
        self.deps('pool', [rin], [rout])
        ins = self.nc.gpsimd.collective_compute("AllGather", ALU.bypass, replica_groups=[list(range(NCORES))],
                                                ins=[in_ap], outs=[out_ap])
        self.tick['pool'] += 1
        ins.then_inc(self.sem['pool'], 1)
        self.commit(('pool', self.sem['pool'], self.tick['pool']), [rin], [rout])

    def barrier(self, engines=('pe', 'act', 'dve', 'sp'), include_pool=False):
        srcs = list(engines) + (['pool'] if include_pool else [])
        for e in list(engines) + ['pool']:
            for s in srcs:
                if s != e and self.tick[s] > 0:
                    self._wait(e, (s, self.sem[s], self.tick[s]))
            for q in (('sp', 'pool') if include_pool else ('sp',)):
                ring = self.ring[q]
                n = self.ringcnt[q]
                for slot in range(len(ring)):
                    cnt = (n - slot + len(ring) - 1) // len(ring) if n > slot else 0
                    if cnt > 0:
                        self._wait(e, (f'd_{q}{slot}', ring[slot], 16 * cnt))


def build_program(upto='all', dumps=()):
    nc = bass.Bass("TRN2", target_bir_lowering=False)
    stack = contextlib.ExitStack()

    def din(name, shape, dt=F32):
        return nc.dram_tensor(name, list(shape), dt, kind="ExternalInput").ap()

    def dint(name, shape, dt=BF16):
        return nc.dram_tensor(name, list(shape), dt, kind="Internal").ap()

    xT_in = din("xT", [32, 128, T])
    xcT_in = din("xcT", [32, 128, NCTX])
    cvec = din("cvec", [128, 32, 2])
    wada = din("wada", [DEPTH, 24, 128, 32, 128])
    bada = din("bada", [DEPTH, 128, 24, 2])
    pvec_in = din("pvec", [DEPTH, 128, PV_N])
    flags_in = din("flags", [128, 2])
    invcnt_in = din("invcnt", [128, 4, NT])
    poolw_in = din("poolw", [DEPTH, 4, 256, 256])
    fnetw_in = din("fnetw", [DEPTH, 4, 256, 256])
    utab_in = din("utab", [DEPTH, NH, 128, 9 * 128])
    rm_in = din("rmask", [128, 8, 7 * 128])
    cs_in = din("cstab", [64, 128, 2, T], BF16)
    csc_in = din("cstabc", [2, 128, 2, NCTX], BF16)
    cdft_in = din("cdft", [128, 2, 2, 256], BF16)
    w_sh = {
        'winA': din("winA", [DEPTH, 5 * 128, 32 * 128]),
        'winB': din("winB", [DEPTH, 128, 32 * 384]),
        'wgate': din("wgate", [DEPTH, 12 * 128, 32 * 128]),
        'wbr': din("wbr", [DEPTH, 4 * 128, 32 * 128]),
        'wo': din("wo", [DEPTH, 4 * 128, 32 * 128]),
        'wup': din("wup", [DEPTH, 86 * 128, 4 * 256]),
        'wdown': din("wdown", [DEPTH, 4 * 128, 86 * 128]),
    }
    out_ap = nc.dram_tensor("outT", [32, 128, T], F32, kind="ExternalOutput").ap()

    w_bf = {k: [dint(f"{k}_bf{l}", v.shape[1:]) for l in range(DEPTH)] for k, v in w_sh.items()}
    w_full = {k: [dint(f"{k}_full{l}", [v.shape[1] * NCORES, v.shape[2]]) for l in range(DEPTH)] for k, v in w_sh.items()}
    mod_part = [dint(f"modp{l}", [128, 48], F32) for l in range(DEPTH)]
    mod_all = [dint(f"moda{l}", [NCORES * 128, 48], F32) for l in range(DEPTH)]
    hT_d = dint("hT_d", [32, 128, NT])
    qT_d = dint("qT_d", [NH, 128, NT])
    kT_d = dint("kT_d", [NH, 128, NT])
    upT_d = dint("upT_d", [8, 128, NT], F32)
    vf_d = dint("vf_d", [NT, 3072])
    kh_in = dint("kh_in", [2 * NH * 128, 256])
    kh_all = dint("kh_all", [NCORES * 2 * NH * 128, 256])
    vh_in = dint("vh_in", [2 * 256, 2048])
    vh_all = dint("vh_all", [NCORES * 2 * 256, 2048])
    uph_in = dint("uph_in", [2 * 8 * 128, 8], F32)
    uph_all = dint("uph_all", [NCORES * 2 * 8 * 128, 8], F32)
    kh_prev = dint("kh_prev", [NH * 128, 256])
    kh_next = dint("kh_next", [NH * 128, 256])
    vh_prev = dint("vh_prev", [256, 2048])
    vh_next = dint("vh_next", [256, 2048])
    uph_prev = dint("uph_prev", [8 * 128, 8], F32)
    uph_next = dint("uph_next", [8 * 128, 8], F32)
    xb_prev = dint("xb_prev", [32, 128], F32)
    xb_next = dint("xb_next", [32, 128], F32)
    uf_in = dint("uf_in", [T, 1024])
    uf_all = dint("uf_all", [SEQ, 1024])
    yT_d = dint("yT_d", [32, 128, NT])
    gate_d = dint("gate_d", [96, 128, NT])
    mT_d = dint("mT_d", [32, 128, NT])
    xmT_d = dint("xmT_d", [32, 128, NT], F32)
    xb_in = dint("xb_in", [2 * 32, 128], F32)
    xb_all = dint("xb_all", [NCORES * 2 * 32, 128], F32)
    aT_d = dint("aT_d", [NPAIR, 128, NT])
    x1T_d = dint("x1T_d", [32, 128, NT], F32)
    scratch = {'hT_d': hT_d, 'qT_d': qT_d, 'kT_d': kT_d, 'upT_d': upT_d, 'vf_d': vf_d, 'yT_d': yT_d,
               'gate_d': gate_d, 'mT_d': mT_d, 'xmT_d': xmT_d, 'aT_d': aT_d, 'x1T_d': x1T_d,
               'moda0': mod_all[0], 'moda1': mod_all[1], 'uf_all': uf_all, 'kh_all': kh_all}
    S1 = [(0, 64), (T - 64, 64), (T, 64)]
    S2 = [(0, 256), (T - 256, 256), (T, 256)]
    S3 = [(0, 128), (T - 128, 128), (T, 256)]
    dump_spec = {
        'hT': (hT_d, 'fm', S1), 'qT': (qT_d, 'fm', S1), 'kT': (kT_d, 'fm', S2), 'v': (vf_d[:, 0:2048], 'tm', S2),
        'uf': (vf_d[:, 2048:3072], 'tm', [(0, NT)]), 'upT': (upT_d, 'fm', S3), 'yT': (yT_d, 'fm', S1), 'mT': (mT_d, 'fm', S1),
        'xmT': (xmT_d, 'fm', S1), 'x1T': (x1T_d, 'fm', S1), 'aT': (aT_d, 'fm', S1), 'gate': (gate_d, 'fm', S1),
        'moda0': (mod_all[0], 'tm', [(0, 1024)]), 'moda1': (mod_all[1], 'tm', [(0, 1024)]),
    }
    dump_regs = {'hT': 'hT_d', 'qT': 'qT_d', 'kT': 'kT_d', 'v': 'vf_d', 'uf': 'vf_d', 'upT': 'upT_d', 'yT': 'yT_d', 'mT': 'mT_d',
                 'xmT': 'xmT_d', 'x1T': 'x1T_d', 'aT': 'aT_d', 'gate': 'gate_d', 'moda0': 'moda0', 'moda1': 'moda1'}
    dump_out = {}
    for nm in dumps:
        src, kind, sel = dump_spec[nm]
        n = sum(w for _, w in sel)
        shp = [src.shape[0], 128, n] if kind == 'fm' else [n, src.shape[1]]
        dump_out[nm] = nc.dram_tensor("dump_" + nm, shp, src.tensor.dtype, kind="ExternalOutput").ap()

    _uid = [0]

    def sbt(name, shape, dt):
        _uid[0] += 1
        return nc.sbuf_tensor(f"{name}_{_uid[0]}", shape, dt)

    R = {}

    def reg(name):
        if name not in R:
            R[name] = Reg(name)
        return R[name]

    with stack:
        S = Sync(nc, stack)
        psum = stack.enter_context(nc.psum_tensor("psum", [128, 8, 512], F32))
        psf = psum[:].rearrange("p b n -> p (b n)")
        PB = [Reg(f'psb{i}') for i in range(8)]
        consts = stack.enter_context(sbt("consts", [128, 256], F32))
        onesb = stack.enter_context(sbt("onesb", [128, 128], BF16))
        flags = stack.enter_context(sbt("flags_sb", [128, 2], F32))
        rconst = reg('consts')
        S.op('dve', lambda: nc.vector.memset(consts[:, 0:128], 1.0), writes=[rconst])
        S.op('dve', lambda: nc.vector.memset(onesb[:], 1.0), writes=[rconst])
        S.op('dve', lambda: nc.vector.memset(consts[:, 128:129], float(EPS * D)), writes=[rconst])
        S.op('dve', lambda: nc.vector.memset(consts[:, 129:130], float(EPS * HD)), writes=[rconst])
        S.dma('sp', flags[:], flags_in, writes=[rconst])
        ones32 = consts[:, 0:128]
        pid = nc.sync.partition_id() % NCORES
        prev_c = (pid + (NCORES - 1)) % NCORES
        next_c = (pid + 1) % NCORES

        def prep_weights(l, names):
            for k in names:
                src = w_sh[k][l]
                dst = w_bf[k][l]
                rows = src.shape[0]
                nsp = max(1, (rows * src.shape[1] * 4) // (8 << 20))
                step = (rows + nsp - 1) // nsp
                step = ((step + 127) // 128) * 128
                r0 = 0
                while r0 < rows:
                    r1 = min(rows, r0 + step)
                    S.dma('pool', dst[r0:r1, :], src[r0:r1, :], writes=[reg(f'wbf_{k}{l}')])
                    r0 = r1
                S.allgather(dst, w_full[k][l], reg(f'wbf_{k}{l}'), reg(f'wfull_{k}{l}'))

        def mod_stage():
            with contextlib.ExitStack() as st:
                sT = st.enter_context(sbt("sT", [128, 32, 2], F32))
                sT2 = st.enter_context(sbt("sT2", [128, 32, 2], F32))
                wt = [st.enter_context(sbt(f"wadat{i}", [128, 32, 128], F32)) for i in range(3)]
                bt = st.enter_context(sbt("badat", [128, 24, 2], F32))
                mo = st.enter_context(sbt("modo", [128, 48], F32))
                rs, rs2, rb, rmo = Reg(), Reg(), Reg(), Reg()
                rw = [Reg() for _ in range(3)]
                S.dma('sp', sT[:], cvec, writes=[rs])
                S.op('act', lambda: nc.scalar.activation(out=sT2[:], in_=sT[:], func=AF.Silu), reads=[rs], writes=[rs2])
                n = 0
                for l in range(DEPTH):
                    S.dma('sp', bt[:], bada[l], writes=[rb])
                    for j in range(24):
                        slot = n % 3
                        n += 1
                        S.dma('sp', wt[slot][:], wada[l, j], writes=[rw[slot]])
                        for kc in range(32):
                            last = kc == 31
                            f = (lambda kc=kc, slot=slot, j=j, last=last: nc.tensor.matmul(
                                psum[:, 0, 2 * j:2 * j + 2], lhsT=wt[slot][:, kc, :], rhs=sT2[:, kc, :],
                                start=(kc == 0), stop=last))
                            if last:
                                S.op('pe', f, reads=[rw[slot], rs2], writes=[PB[0]])
                            else:
                                S.op_nosig('pe', f, reads=[rw[slot], rs2], writes=[PB[0]])
                    S.op('dve', lambda: nc.vector.tensor_tensor(out=mo[:], in0=psum[:, 0, 0:48], in1=bt[:].rearrange("p j t -> p (j t)"), op=ALU.add),
                         reads=[PB[0], rb], writes=[rmo])
                    S.dma('sp', mod_part[l], mo[:], reads=[rmo], writes=[reg(f'modp{l}')])
                    S.allgather(mod_part[l], mod_all[l], reg(f'modp{l}'), reg(f'moda{l}'))
            S.barrier()

        def load_mod(st, l, which):
            msb = st.enter_context(sbt(f"modsb{which}", [128, 192, 2], F32))
            pv = st.enter_context(sbt(f"pvsb{which}", [128, PV_N], F32))
            AB = st.enter_context(sbt(f"AB{which}", [128, 6, 32], F32))
            rm_, rp_, rab = Reg(), Reg(), Reg()
            S.dma('sp', msb[:].rearrange("p (r j) t -> p r (j t)", r=8), mod_all[l].rearrange("(r p) c -> p r c", p=128),
                  reads=[reg(f'moda{l}')], writes=[rm_])
            S.dma('sp', pv[:], pvec_in[l], writes=[rp_])
            base = 96 * which
            nwo = PV_N1 if which == 0 else PV_N2
            for t in range(2):
                S.op('dve', lambda t=t: nc.vector.scalar_tensor_tensor(
                    out=AB[:, 0 + t, :], in0=msb[:, base + 32:base + 64, t], scalar=1.0, in1=pv[:, nwo:nwo + 32],
                    op0=ALU.add, op1=ALU.mult), reads=[rm_, rp_], writes=[rab])
                S.op('dve', lambda t=t: nc.vector.tensor_scalar(
                    out=AB[:, 0 + t, :], in0=AB[:, 0 + t, :], scalar1=float(np.sqrt(D)), scalar2=None, op0=ALU.mult),
                    reads=[rab], writes=[rab])
                S.op('dve', lambda t=t: nc.vector.tensor_copy(out=AB[:, 2 + t, :], in_=msb[:, base:base + 32, t]), reads=[rm_], writes=[rab])
                S.op('dve', lambda t=t: nc.vector.tensor_copy(out=AB[:, 4 + t, :], in_=msb[:, base + 64:base + 96, t]), reads=[rm_], writes=[rab])
            return AB, pv, rab, rp_

        def norm_stage(st, l, which, src_lat, src_ctx, hT, rh, do_ctx, halo=None):
            AB, pv, rab, rp_ = load_mod(st, l, which)
            xt = [st.enter_context(sbt(f"xt{which}{i}", [128, 32, 256], F32)) for i in range(2)]
            sq = [st.enter_context(sbt(f"sq{which}{i}", [128, 256], F32)) for i in range(2)]
            rsb = [st.enter_context(sbt(f"rs{which}{i}", [128, 256], F32)) for i in range(2)]
            tmp = [st.enter_context(sbt(f"tmp{which}{i}", [128, 256], F32)) for i in range(2)]
            rxt = [Reg(), Reg()]
            rsq = [Reg(), Reg()]
            rrs = [Reg(), Reg()]
            rtmp = [Reg(), Reg()]
            tiles = [(src_lat[:, :, i * 256:(i + 1) * 256], i * 256, 256, 0) for i in range(4)]
            if do_ctx:
                tiles.append((src_ctx[:, :, 0:256], T, 256, 1))
            if halo is not None:
                tiles.append((None, NT, 2, 0))
            for ti, (src, c0, w, typ) in enumerate(tiles):
                b = ti % 2
                if src is not None:
                    S.dma('sp', xt[b][:, :, 0:w], src.rearrange("f p t -> p f t"), writes=[rxt[b]])
                else:
                    xb, rxb = halo[0], halo[1]
                    for q4 in range(4):
                        S.dma('sp', xt[b][:, 8 * q4:8 * q4 + 8, 0:1], xb_prev[8 * q4:8 * q4 + 8, :].rearrange("f (p o) -> p f o", o=1), reads=[rxb], writes=[rxt[b]],
                              allow_slow_non_contiguous=True)
                        S.dma('sp', xt[b][:, 8 * q4:8 * q4 + 8, 1:2], xb_next[8 * q4:8 * q4 + 8, :].rearrange("f (p o) -> p f o", o=1), reads=[rxb], writes=[rxt[b]],
                              allow_slow_non_contiguous=True)
                pb = 6 + b
                for fc in range(32):
                    sb_ = fc % 2
                    S.op('act', lambda fc=fc, sb_=sb_, b=b, w=w: nc.scalar.activation(out=sq[sb_][:, 0:w], in_=xt[b][:, fc, 0:w], func=AF.Square),
                         reads=[rxt[b]], writes=[rsq[sb_]])
                    f = (lambda fc=fc, sb_=sb_, pb=pb, w=w: nc.tensor.matmul(psum[:, pb, 0:w], lhsT=ones32, rhs=sq[sb_][:, 0:w],
                                                                            start=(fc == 0), stop=(fc == 31)))
                    S.op('pe', f, reads=[rsq[sb_], rconst], writes=[PB[pb]])
                S.op('act', lambda b=b, pb=pb, w=w: nc.scalar.activation(out=rsb[b][:, 0:w], in_=psum[:, pb, 0:w], func=AF.Sqrt, bias=consts[:, 128:129], scale=1.0),
                     reads=[PB[pb], rconst], writes=[rrs[b]])
                S.op('dve', lambda b=b, w=w: nc.vector.reciprocal(out=rsb[b][:, 0:w], in_=rsb[b][:, 0:w]), reads=[rrs[b]], writes=[rrs[b]])
                for fc in range(32):
                    tb = fc % 2
                    S.op('dve', lambda fc=fc, tb=tb, b=b, w=w, typ=typ: nc.vector.scalar_tensor_tensor(
                        out=tmp[tb][:, 0:w], in0=xt[b][:, fc, 0:w], scalar=AB[:, 0 + typ, fc:fc + 1], in1=rsb[b][:, 0:w],
                        op0=ALU.mult, op1=ALU.mult), reads=[rxt[b], rrs[b], rab], writes=[rtmp[tb]])
                    S.op('act', lambda fc=fc, tb=tb, c0=c0, w=w, typ=typ: nc.scalar.activation(
                        out=hT[:, fc, c0:c0 + w], in_=tmp[tb][:, 0:w], func=AF.Identity, bias=AB[:, 2 + typ, fc:fc + 1], scale=1.0),
                        reads=[rtmp[tb], rab], writes=[rh])
                if src is None:
                    for o in range(2):
                        S.op('dve', lambda o=o, c0=c0: nc.vector.tensor_scalar(
                            out=hT[:, :, c0 + o], in0=hT[:, :, c0 + o], scalar1=flags[:, o:o + 1], scalar2=None, op0=ALU.mult),
                            reads=[rh, rconst], writes=[rh])
            return AB, pv, rab, rp_

        def stage_A(l, src_lat, src_ctx):
            with contextlib.ExitStack() as st:
                hT = st.enter_context(sbt("hT", [128, 32, NT], BF16))
                rh = Reg('hT')
                with contextlib.ExitStack() as st2:
                    AB, pv, rab, rp_ = norm_stage(st2, l, 0, src_lat, src_ctx, hT, rh, True)
                    S.barrier()
                for i in range(4):
                    S.dma('sp', hT_d[8 * i:8 * i + 8].rearrange("f p t -> p f t"), hT[:, 8 * i:8 * i + 8, :], reads=[rh], writes=[reg('hT_d')])
                pv = st.enter_context(sbt("pvA", [128, PV_N], F32))
                qk = st.enter_context(sbt("qkw", [128, 2], F32))
                rpv = Reg()
                S.dma('sp', pv[:], pvec_in[l], writes=[rpv])
                S.op('dve', lambda: nc.vector.tensor_scalar(out=qk[:, 0:1], in0=pv[:, PV_QW:PV_QW + 1], scalar1=float(np.sqrt(HD) * HD ** -0.5), scalar2=None, op0=ALU.mult),
                     reads=[rpv], writes=[rpv])
                S.op('dve', lambda: nc.vector.tensor_scalar(out=qk[:, 1:2], in0=pv[:, PV_KW:PV_KW + 1], scalar1=float(np.sqrt(HD)), scalar2=None, op0=ALU.mult),
                     reads=[rpv], writes=[rpv])
                with contextlib.ExitStack() as st2:
                    wt = [st2.enter_context(sbt(f"wA{i}", [128, 32, 128], BF16)) for i in range(3)]
                    rw = [Reg() for _ in range(3)]
                    o32 = [st2.enter_context(sbt(f"o32_{i}", [128, NT], F32)) for i in range(2)]
                    o16 = [st2.enter_context(sbt(f"o16_{i}", [128, NT], BF16)) for i in range(2)]
                    sqh = [st2.enter_context(sbt(f"sqh{i}", [128, 512], F32)) for i in range(2)]
                    rsh = [st2.enter_context(sbt(f"rsh{i}", [128, 512], F32)) for i in range(2)]
                    ro32 = [Reg(), Reg()]
                    ro16 = [Reg(), Reg()]
                    rsqh = [Reg(), Reg()]
                    rrsh = [Reg(), Reg()]
                    wfull = w_full['winA'][l]
                    rwf = reg(f'wfull_winA{l}')
                    tts = [(0, 512), (512, 512), (1024, 256)]
                    NOC = 40

                    def loadw(oc):
                        S.dma('sp', wt[oc % 3][:].rearrange("p k n -> p (k n)"), wfull[oc * 128:(oc + 1) * 128, :], reads=[rwf], writes=[rw[oc % 3]])
                    loadw(0)
                    loadw(1)
                    cnt = 0
                    for oc in range(NOC):
                        if oc + 2 < NOC:
                            loadw(oc + 2)
                        ws = oc % 3
                        ob = oc % 2
                        for ti, (c0, w) in enumerate(tts):
                            pb = (cnt % 2) * 3 + ti
                            for kc in range(32):
                                f = (lambda kc=kc, pb=pb, ws=ws, c0=c0, w=w: nc.tensor.matmul(
                                    psum[:, pb, 0:w], lhsT=wt[ws][:, kc, :], rhs=hT[:, kc, c0:c0 + w], start=(kc == 0), stop=(kc == 31)))
                                if kc == 31:
                                    S.op('pe', f, reads=[rw[ws], rh], writes=[PB[pb]])
                                else:
                                    S.op_nosig('pe', f, reads=[rw[ws], rh], writes=[PB[pb]])
                            if oc < 8:
                                S.op('act', lambda pb=pb, ob=ob, c0=c0, w=w: nc.scalar.activation(out=o32[ob][:, c0:c0 + w], in_=psum[:, pb, 0:w], func=AF.Identity),
                                     reads=[PB[pb]], writes=[ro32[ob]])
                            else:
                                isq = 0 if oc < 24 else 1
                                sb_ = ti % 2
                                pb2 = 6 + sb_
                                S.op('act', lambda pb=pb, sb_=sb_, w=w: nc.scalar.activation(out=sqh[sb_][:, 0:w], in_=psum[:, pb, 0:w], func=AF.Square),
                                     reads=[PB[pb]], writes=[rsqh[sb_]])
                                S.op('pe', lambda pb2=pb2, sb_=sb_, w=w: nc.tensor.matmul(psum[:, pb2, 0:w], lhsT=ones32, rhs=sqh[sb_][:, 0:w], start=True, stop=True),
                                     reads=[rsqh[sb_], rconst], writes=[PB[pb2]])
                                S.op('act', lambda pb2=pb2, sb_=sb_, w=w: nc.scalar.activation(out=rsh[sb_][:, 0:w], in_=psum[:, pb2, 0:w], func=AF.Sqrt, bias=consts[:, 129:130], scale=1.0),
                                     reads=[PB[pb2], rconst], writes=[rrsh[sb_]])
                                S.op('dve', lambda sb_=sb_, w=w: nc.vector.reciprocal(out=rsh[sb_][:, 0:w], in_=rsh[sb_][:, 0:w]), reads=[rrsh[sb_]], writes=[rrsh[sb_]])
                                S.op('dve', lambda pb=pb, sb_=sb_, ob=ob, c0=c0, w=w, isq=isq: nc.vector.scalar_tensor_tensor(
                                    out=o16[ob][:, c0:c0 + w], in0=psum[:, pb, 0:w], scalar=qk[:, isq:isq + 1], in1=rsh[sb_][:, 0:w], op0=ALU.mult, op1=ALU.mult),
                                    reads=[PB[pb], rrsh[sb_], rpv], writes=[ro16[ob]])
                        cnt += 1
                        if oc < 8:
                            S.dma('sp', upT_d[oc], o32[ob][:], reads=[ro32[ob]], writes=[reg('upT_d')])
                        elif oc < 24:
                            S.dma('sp', qT_d[oc - 8], o16[ob][:], reads=[ro16[ob]], writes=[reg('qT_d')])
                        else:
                            S.dma('sp', kT_d[oc - 24], o16[ob][:], reads=[ro16[ob]], writes=[reg('kT_d')])
                    S.barrier()
                with contextlib.ExitStack() as st2:
                    wt = [st2.enter_context(sbt(f"wB{i}", [128, 32, 384], BF16)) for i in range(2)]
                    rw = [Reg() for _ in range(2)]
                    ov = [st2.enter_context(sbt(f"ov{i}", [128, 384], BF16)) for i in range(3)]
                    rov = [Reg() for _ in range(3)]
                    wfull = w_full['winB'][l]
                    rwf = reg(f'wfull_winB{l}')

                    def loadw(g):
                        S.dma('sp', wt[g % 2][:].rearrange("p k n -> p (k n)"), wfull[g * 128:(g + 1) * 128, :], reads=[rwf], writes=[rw[g % 2]])
                    loadw(0)
                    cnt = 0
                    for g in range(8):
                        if g + 1 < 8:
                            loadw(g + 1)
                        ws = g % 2
                        for tb in range(NT // 128):
                            pb = cnt % 6
                            for kc in range(32):
                                f = (lambda kc=kc, pb=pb, ws=ws, tb=tb: nc.tensor.matmul(
                                    psum[:, pb, 0:384], lhsT=hT[:, kc, tb * 128:(tb + 1) * 128], rhs=wt[ws][:, kc, :], start=(kc == 0), stop=(kc == 31)))
                                if kc == 31:
                                    S.op('pe', f, reads=[rw[ws], rh], writes=[PB[pb]])
                                else:
                                    S.op_nosig('pe', f, reads=[rw[ws], rh], writes=[PB[pb]])
                            os_ = cnt % 3
                            eng = 'act' if cnt % 2 == 0 else 'dve'
                            if eng == 'act':
                                S.op('act', lambda pb=pb, os_=os_: nc.scalar.activation(out=ov[os_][:], in_=psum[:, pb, 0:384], func=AF.Identity),
                                     reads=[PB[pb]], writes=[rov[os_]])
                            else:
                                S.op('dve', lambda pb=pb, os_=os_: nc.vector.tensor_copy(out=ov[os_][:], in_=psum[:, pb, 0:384]),
                                     reads=[PB[pb]], writes=[rov[os_]])
                            S.dma('sp', vf_d[tb * 128:(tb + 1) * 128, g * 384:(g + 1) * 384], ov[os_][:], reads=[rov[os_]], writes=[reg('vf_d')])
                            cnt += 1
                    S.barrier()

        def exchange1():
            rk, rv, ru = reg('kT_d'), reg('vf_d'), reg('upT_d')
            for i, c0 in enumerate((0, T - 256)):
                S.dma('sp', kh_in[i * NH * 128:(i + 1) * NH * 128, :].rearrange("(h p) t -> h p t", p=128), kT_d[:, :, c0:c0 + 256], reads=[rk], writes=[reg('kh_in')])
                S.dma('sp', vh_in[i * 256:(i + 1) * 256, :], vf_d[c0:c0 + 256, 0:2048], reads=[rv], writes=[reg('vh_in')])
            for i, c0 in enumerate((0, T - 8)):
                S.dma('sp', uph_in[i * 1024:(i + 1) * 1024, :].rearrange("(f p) t -> f p t", p=128), upT_d[:, :, c0:c0 + 8], reads=[ru], writes=[reg('uph_in')])
            S.dma('sp', uf_in, vf_d[0:T, 2048:3072], reads=[rv], writes=[reg('uf_in')])
            S.allgather(kh_in, kh_all, reg('kh_in'), reg('kh_all'))
            S.allgather(vh_in, vh_all, reg('vh_in'), reg('vh_all'))
            S.allgather(uph_in, uph_all, reg('uph_in'), reg('uph_all'))
            S.allgather(uf_in, uf_all, reg('uf_in'), reg('uf_all'))
            S.dma('sp', kh_prev, kh_all[bass.ds(prev_c * 4096 + 2048, 2048), :], reads=[reg('kh_all')], writes=[reg('kh_halo')])
            S.dma('sp', kh_next, kh_all[bass.ds(next_c * 4096, 2048), :], reads=[reg('kh_all')], writes=[reg('kh_halo')])
            S.dma('sp', vh_prev, vh_all[bass.ds(prev_c * 512 + 256, 256), :], reads=[reg('vh_all')], writes=[reg('vh_halo')])
            S.dma('sp', vh_next, vh_all[bass.ds(next_c * 512, 256), :], reads=[reg('vh_all')], writes=[reg('vh_halo')])
            S.dma('sp', uph_prev, uph_all[bass.ds(prev_c * 2048 + 1024, 1024), :], reads=[reg('uph_all')], writes=[reg('uph_halo')])
            S.dma('sp', uph_next, uph_all[bass.ds(next_c * 2048, 1024), :], reads=[reg('uph_all')], writes=[reg('uph_halo')])

        def stage_attn(l, do_ctx):
            with contextlib.ExitStack() as st:
                rmsb = st.enter_context(sbt("rmsb", [128, 8, 7 * 128], F32))
                rrm = Reg()
                S.dma('sp', rmsb[:], rm_in, writes=[rrm])
                Kl = [st.enter_context(sbt(f"Kl{i}", [128, 1536], BF16)) for i in range(2)]
                Kc = [st.enter_context(sbt(f"Kc{i}", [128, 256], BF16)) for i in range(2)]
                Vl = [st.enter_context(sbt(f"Vl{i}", [128, 14, 128], BF16)) for i in range(2)]
                Ql = [st.enter_context(sbt(f"Ql{i}", [128, NT], BF16)) for i in range(2)]
                Ut = [st.enter_context(sbt(f"Ut{i}", [128, 9 * 128], F32)) for i in range(2)]
                Yh = [st.enter_context(sbt(f"Yh{i}", [128, NT], BF16)) for i in range(2)]
                sS = [st.enter_context(sbt(f"sS{i}", [128, 896], F32)) for i in range(2)]
                Pt = [st.enter_context(sbt(f"Pt{i}", [128, 1152], BF16)) for i in range(2)]
                rden = [st.enter_context(sbt(f"rden{i}", [128, 128], F32)) for i in range(2)]
                rin = [Reg(), Reg()]
                rY = [Reg(), Reg()]
                rsS = [Reg(), Reg()]
                rP = [Reg(), Reg()]
                rrd = [Reg(), Reg()]
                PS_S = [[PB[0], PB[1], PB[2]], [PB[3], PB[4], PB[5]]]

                def load_head(h):
                    b = h % 2
                    w_ = [rin[b]]
                    S.dma('sp', Kl[b][:, 0:256], kh_prev[h * 128:(h + 1) * 128, :], reads=[reg('kh_halo')], writes=w_)
                    S.dma('sp', Kl[b][:, 256:1280], kT_d[h, :, 0:T], reads=[reg('kT_d')], writes=w_)
                    S.dma('sp', Kl[b][:, 1280:1536], kh_next[h * 128:(h + 1) * 128, :], reads=[reg('kh_halo')], writes=w_)
                    S.dma('sp', Kc[b][:], kT_d[h, :, T:NT], reads=[reg('kT_d')], writes=w_)
                    S.dma('sp', Vl[b][:, 0:2, :], vh_prev[:, h * 128:(h + 1) * 128].rearrange("(b p) d -> p b d", p=128),
                          reads=[reg('vh_halo')], writes=w_)
                    S.dma('sp', Vl[b][:, 2:10, :], vf_d[0:T, h * 128:(h + 1) * 128].rearrange("(b p) d -> p b d", p=128), reads=[reg('vf_d')], writes=w_)
                    S.dma('sp', Vl[b][:, 10:12, :], vh_next[:, h * 128:(h + 1) * 128].rearrange("(b p) d -> p b d", p=128),
                          reads=[reg('vh_halo')], writes=w_)
                    S.dma('sp', Vl[b][:, 12:14, :], vf_d[T:NT, h * 128:(h + 1) * 128].rearrange("(b p) d -> p b d", p=128), reads=[reg('vf_d')], writes=w_)
                    S.dma('sp', Ql[b][:], qT_d[h], reads=[reg('qT_d')], writes=w_)
                    S.dma('sp', Ut[b][:], utab_in[l, h], writes=w_)

                load_head(0)
                cnt = 0
                for h in range(NH):
                    if h + 1 < NH:
                        load_head(h + 1)
                    b = h % 2
                    units = [(p, False) for p in range(8)] + ([(qt, True) for qt in range(2)] if do_ctx else [])
                    for (p, isctx) in units:
                        sb_ = cnt % 2
                        cnt += 1
                        psS = psf[:, sb_ * 1536: sb_ * 1536 + 1152]
                        psO = psum[:, 6 + sb_, :]
                        rS = PS_S[sb_]
                        rO = PB[6 + sb_]
                        if not isctx:
                            rs_ = 2 * p if p <= 5 else 10
                            e0 = 2 if p <= 5 else (1 if p == 6 else 0)
                            q0 = p * 128
                            nk = 9
                            lhs_k = [Kl[b][:, (rs_ + 2 * m) * 64:(rs_ + 2 * m) * 64 + 128] for m in range(7)] + [Kc[b][:, j * 128:(j + 1) * 128] for j in range(2)]
                            lhs_v = [Vl[b][:, rs_ // 2 + m, :] for m in range(7)] + [Vl[b][:, 12 + j, :] for j in range(2)]
                        else:
                            q0 = T + p * 128
                            nk = 2
                            lhs_k = [Kc[b][:, j * 128:(j + 1) * 128] for j in range(2)]
                            lhs_v = [Vl[b][:, 12 + j, :] for j in range(2)]
                        for m in range(nk):
                            f = (lambda m=m, lk=lhs_k[m], q0=q0, psS=psS: nc.tensor.matmul(
                                psS[:, m * 128:(m + 1) * 128], lhsT=lk, rhs=Ql[b][:, q0:q0 + 128], start=True, stop=True))
                            if m == nk - 1:
                                S.op('pe', f, reads=[rin[b]], writes=rS)
                            else:
                                S.op_nosig('pe', f, reads=[rin[b]], writes=rS)
                        if not isctx:
                            S.op('dve', lambda psS=psS, sb_=sb_, e0=e0: nc.vector.tensor_tensor(out=sS[sb_][:], in0=psS[:, 0:896], in1=Ut[b][:, e0 * 128:(e0 + 7) * 128], op=ALU.add),
                                 reads=rS + [rin[b]], writes=[rsS[sb_]])
                            S.op('dve', lambda sb_=sb_, p=p: nc.vector.tensor_tensor(out=sS[sb_][:], in0=sS[sb_][:], in1=rmsb[:, p, :], op=ALU.add),
                                 reads=[rsS[sb_], rrm], writes=[rsS[sb_]])
                            S.op('act', lambda sb_=sb_: nc.scalar.activation(out=Pt[sb_][:, 0:896], in_=sS[sb_][:], func=AF.Exp), reads=[rsS[sb_]], writes=[rP[sb_]])
                            S.op('act', lambda sb_=sb_, psS=psS: nc.scalar.activation(out=Pt[sb_][:, 896:1152], in_=psS[:, 896:1152], func=AF.Exp), reads=rS, writes=[rP[sb_]])
                        else:
                            S.op('act', lambda sb_=sb_, psS=psS: nc.scalar.activation(out=Pt[sb_][:, 0:256], in_=psS[:, 0:256], func=AF.Exp), reads=rS, writes=[rP[sb_]])
                        for m in range(nk):
                            f = (lambda m=m, lv=lhs_v[m], psO=psO, sb_=sb_, nk=nk: nc.tensor.matmul(psO[:, 0:128], lhsT=lv, rhs=Pt[sb_][:, m * 128:(m + 1) * 128],
                                                                                                 start=(m == 0), stop=(m == nk - 1)))
                            S.op_nosig('pe', f, reads=[rin[b], rP[sb_]], writes=[rO])
                        for m in range(nk):
                            f = (lambda m=m, psO=psO, sb_=sb_, nk=nk: nc.tensor.matmul(psO[:, 128:256], lhsT=onesb[:], rhs=Pt[sb_][:, m * 128:(m + 1) * 128],
                                                                                      start=(m == 0), stop=(m == nk - 1)))
                            if m == nk - 1:
                                S.op('pe', f, reads=[rin[b], rP[sb_], rconst], writes=[rO])
                            else:
                                S.op_nosig('pe', f, reads=[rin[b], rP[sb_], rconst], writes=[rO])
                        S.op('dve', lambda psO=psO, sb_=sb_: nc.vector.reciprocal(out=rden[sb_][:], in_=psO[:, 128:256]), reads=[rO], writes=[rrd[sb_]])
                        S.op('dve', lambda psO=psO, sb_=sb_, q0=q0: nc.vector.tensor_tensor(out=Yh[b][:, q0:q0 + 128], in0=psO[:, 0:128], in1=rden[sb_][:], op=ALU.mult),
                             reads=[rO, rrd[sb_]], writes=[rY[b]])
                    wcols = NT if do_ctx else T
                    S.dma('sp', yT_d[8 + h, :, 0:wcols], Yh[b][:, 0:wcols], reads=[rY[b]], writes=[reg('yT_d')])
                S.barrier()

        def stage_pool(l, do_ctx):
            with contextlib.ExitStack() as st:
                pw = st.enter_context(sbt("pw", [128, 4, 2, 256], BF16))
                pv = st.enter_context(sbt("pvP", [128, PV_N], F32))
                rpw, rpv = Reg(), Reg()
                S.dma('pool', pw[:], poolw_in[l].rearrange("g (c p) e -> p g c e", p=128), writes=[rpw])
                S.dma('sp', pv[:], pvec_in[l], writes=[rpv])
                W = T + 16
                WC = NCTX + 16
                U = [st.enter_context(sbt(f"U{i}", [128, W + WC], F32)) for i in range(2)]
                A_ = [st.enter_context(sbt(f"A_{i}", [128, W + WC], F32)) for i in range(2)]
                B_ = [st.enter_context(sbt(f"B_{i}", [128, W + WC], F32)) for i in range(2)]
                ic = st.enter_context(sbt("ic", [128, NT], F32))
                dT = st.enter_context(sbt("dT", [128, 2, NT], BF16))
                yo = [st.enter_context(sbt(f"yo{i}", [128, NT], BF16)) for i in range(2)]
                rU = [Reg(), Reg()]
                rA = [Reg(), Reg()]
                rB = [Reg(), Reg()]
                ric, rd = Reg(), Reg()
                ryo = [Reg(), Reg()]
                segs = [(0, W, 0, T)] + ([(W, WC, T, NCTX)] if do_ctx else [])
                cnt = 0
                for g in range(4):
                    S.dma('sp', ic[:], invcnt_in[:, g, :], writes=[ric])
                    for cc in range(2):
                        fc = 2 * g + cc
                        u = U[cc]
                        S.op('dve', lambda u=u: nc.vector.memset(u[:], 0.0), writes=[rU[cc]])
                        S.dma('sp', u[:, 0:8], uph_prev[fc * 128:(fc + 1) * 128, :], reads=[reg('uph_halo')], writes=[rU[cc]])
                        S.dma('sp', u[:, 8:8 + T], upT_d[fc, :, 0:T], reads=[reg('upT_d')], writes=[rU[cc]])
                        S.dma('sp', u[:, 8 + T:16 + T], uph_next[fc * 128:(fc + 1) * 128, :], reads=[reg('uph_halo')], writes=[rU[cc]])
                        if do_ctx:
                            S.dma('sp', u[:, W + 8:W + 8 + NCTX], upT_d[fc, :, T:NT], reads=[reg('upT_d')], writes=[rU[cc]])
                        S.op('dve', lambda u=u: nc.vector.tensor_scalar(out=u[:, 0:8], in0=u[:, 0:8], scalar1=flags[:, 0:1], scalar2=None, op0=ALU.mult),
                             reads=[rU[cc], rconst], writes=[rU[cc]])
                        S.op('dve', lambda u=u: nc.vector.tensor_scalar(out=u[:, 8 + T:16 + T], in0=u[:, 8 + T:16 + T], scalar1=flags[:, 1:2], scalar2=None, op0=ALU.mult),
                             reads=[rU[cc], rconst], writes=[rU[cc]])
                        a, bb = A_[cc], B_[cc]
                        for (o, w, c0, ntok) in segs:
                            S.op('dve', lambda a=a, u=u, o=o, w=w: nc.vector.tensor_tensor(out=a[:, o + 1:o + w], in0=u[:, o:o + w - 1], in1=u[:, o + 1:o + w], op=ALU.add),
                                 reads=[rU[cc]], writes=[rA[cc]])
                            cur, nxt_, rc, rn = a, bb, rA[cc], rB[cc]
                            sh = 1
                            for step in range(g):
                                lo = 2 * sh
                                S.op('dve', lambda cur=cur, nxt_=nxt_, o=o, w=w, sh=sh, lo=lo: nc.vector.tensor_tensor(
                                    out=nxt_[:, o + lo:o + w - lo], in0=cur[:, o + lo - sh:o + w - lo - sh], in1=cur[:, o + lo + sh:o + w - lo + sh], op=ALU.add),
                                    reads=[rc], writes=[rn])
                                cur, nxt_, rc, rn = nxt_, cur, rn, rc
                                sh *= 2
                            S.op('dve', lambda cur=cur, nxt_=nxt_, o=o, c0=c0, ntok=ntok: nc.vector.tensor_tensor(
                                out=nxt_[:, o + 8:o + 8 + ntok], in0=cur[:, o + 8:o + 8 + ntok], in1=ic[:, c0:c0 + ntok], op=ALU.mult), reads=[rc, ric], writes=[rn])
                            S.op('dve', lambda nxt_=nxt_, u=u, o=o, c0=c0, ntok=ntok, cc=cc: nc.vector.tensor_tensor(
                                out=dT[:, cc, c0:c0 + ntok], in0=nxt_[:, o + 8:o + 8 + ntok], in1=u[:, o + 8:o + 8 + ntok], op=ALU.subtract), reads=[rn, rU[cc]], writes=[rd])
                    tts = [(0, 512), (512, 512)] + ([(1024, 256)] if do_ctx else [])
                    for ec in range(2):
                        ob = cnt % 2
                        for ti, (c0, w) in enumerate(tts):
                            pb = (cnt % 2) * 3 + ti
                            for cc in range(2):
                                f = (lambda cc=cc, pb=pb, c0=c0, w=w, ec=ec, g=g: nc.tensor.matmul(psum[:, pb, 0:w], lhsT=pw[:, g, cc, ec * 128:(ec + 1) * 128], rhs=dT[:, cc, c0:c0 + w],
                                                                                                start=(cc == 0), stop=(cc == 1)))
                                if cc == 1:
                                    S.op('pe', f, reads=[rpw, rd], writes=[PB[pb]])
                                else:
                                    S.op_nosig('pe', f, reads=[rpw, rd], writes=[PB[pb]])
                            S.op('act', lambda pb=pb, ob=ob, c0=c0, w=w, g=g, ec=ec: nc.scalar.activation(
                                out=yo[ob][:, c0:c0 + w], in_=psum[:, pb, 0:w], func=AF.Identity, scale=pv[:, PV_PS + 2 * g + ec:PV_PS + 2 * g + ec + 1]),
                                reads=[PB[pb], rpv], writes=[ryo[ob]])
                        wcols = NT if do_ctx else T
                        S.dma('sp', yT_d[2 * g + ec, :, 0:wcols], yo[ob][:, 0:wcols], reads=[ryo[ob]], writes=[reg('yT_d')])
                        cnt += 1
                S.barrier(include_pool=False)
                S._wait('sp', rpw.writer)

        def stage_fnet(l, do_ctx):
            with contextlib.ExitStack() as st:
                fw_ = st.enter_context(sbt("fw_", [128, 4, 2, 256], BF16))
                cd = st.enter_context(sbt("cd", [128, 2, 2, 256], BF16))
                rfw, rcd = Reg(), Reg()
                S.dma('pool', fw_[:], fnetw_in[l].rearrange("g (c p) e -> p g c e", p=128), writes=[rfw])
                S.dma('sp', cd[:], cdft_in, writes=[rcd])
                PQ = st.enter_context(sbt("PQ", [128, 2, 8, T], BF16))
                fT = st.enter_context(sbt("fT", [128, 8, T], BF16))
                yo = [st.enter_context(sbt(f"yof{i}", [128, T], BF16)) for i in range(2)]
                UF = [st.enter_context(sbt(f"UF{i}", [128, 64, 128], BF16)) for i in range(2)]
                CS = [st.enter_context(sbt(f"CS{i}", [128, 2, T], BF16)) for i in range(4)]
                rPQ, rfT = Reg(), Reg()
                ryo = [Reg(), Reg()]
                rUF = [Reg(), Reg()]
                rCS = [Reg() for _ in range(4)]

                def run(S_len, uf_src, ruf, cs_src, k0, nk, scale):
                    nsb = S_len // 128
                    ktiles = [(i * 512, min(512, nk - i * 512)) for i in range((nk + 511) // 512)]
                    nkt = len(ktiles)

                    def load_uf(q8):
                        b = q8 % 2
                        step = max(1, nsb // 4)
                        for s0 in range(0, nsb, step):
                            S.dma('sp', UF[b][:, s0:s0 + step, :], uf_src[s0 * 128:(s0 + step) * 128, q8 * 128:(q8 + 1) * 128].rearrange("(s p) c -> p s c", p=128),
                                  reads=[ruf], writes=[rUF[b]])
                    csn = [0]

                    def load_cs(sb):
                        i = csn[0] % 4
                        csn[0] += 1
                        S.dma('sp', CS[i][:, :, 0:nk], cs_src[sb], writes=[rCS[i]])
                        return i
                    load_uf(0)
                    for q8 in range(8):
                        if q8 + 1 < 8:
                            load_uf(q8 + 1)
                        b = q8 % 2
                        pend = [load_cs(0)]
                        if nsb > 1:
                            pend.append(load_cs(1))
                        for sb in range(nsb):
                            if sb + 2 < nsb:
                                pend.append(load_cs(sb + 2))
                            ci = pend.pop(0)
                            for t2 in range(2):
                                for ki, (kk, kw) in enumerate(ktiles):
                                    pb = t2 * 2 + ki
                                    f = (lambda sb=sb, ci=ci, t2=t2, kk=kk, kw=kw, pb=pb, b=b: nc.tensor.matmul(
                                        psum[:, pb, 0:kw], lhsT=UF[b][:, sb, :], rhs=CS[ci][:, t2, kk:kk + kw], start=(sb == 0), stop=(sb == nsb - 1)))
                                    if (t2 == 1 and ki == nkt - 1) or sb == nsb - 1:
                                        S.op('pe', f, reads=[rUF[b], rCS[ci]], writes=[PB[pb]])
                                    else:
                                        S.op_nosig('pe', f, reads=[rUF[b], rCS[ci]], writes=[PB[pb]])
                        for t2 in range(2):
                            for ki, (kk, kw) in enumerate(ktiles):
                                pb = t2 * 2 + ki
                                if (t2 + ki) % 2 == 0:
                                    S.op('act', lambda t2=t2, kk=kk, kw=kw, pb=pb, q8=q8: nc.scalar.activation(out=PQ[:, t2, q8, kk:kk + kw], in_=psum[:, pb, 0:kw], func=AF.Identity),
                                         reads=[PB[pb]], writes=[rPQ])
                                else:
                                    S.op('dve', lambda t2=t2, kk=kk, kw=kw, pb=pb, q8=q8: nc.vector.tensor_copy(out=PQ[:, t2, q8, kk:kk + kw], in_=psum[:, pb, 0:kw]),
                                         reads=[PB[pb]], writes=[rPQ])
                    cnt = 0
                    for g in range(4):
                        for jc in range(2):
                            for ki, (kk, kw) in enumerate(ktiles):
                                pb = 4 + (cnt % 4)
                                cnt += 1
                                n = 0
                                for t2 in range(2):
                                    for cc in range(2):
                                        f = (lambda t2=t2, cc=cc, g=g, jc=jc, kk=kk, kw=kw, pb=pb, n=n: nc.tensor.matmul(
                                            psum[:, pb, 0:kw], lhsT=cd[:, t2, cc, jc * 128:(jc + 1) * 128], rhs=PQ[:, t2, g * 2 + cc, kk:kk + kw], start=(n == 0), stop=(n == 3)))
                                        if n == 3:
                                            S.op('pe', f, reads=[rcd, rPQ], writes=[PB[pb]])
                                        else:
                                            S.op_nosig('pe', f, reads=[rcd, rPQ], writes=[PB[pb]])
                                        n += 1
                                S.op('act', lambda g=g, jc=jc, kk=kk, kw=kw, pb=pb: nc.scalar.activation(out=fT[:, g * 2 + jc, kk:kk + kw], in_=psum[:, pb, 0:kw], func=AF.Identity, scale=float(scale)),
                                     reads=[PB[pb]], writes=[rfT])
                    for g in range(4):
                        for ec in range(2):
                            ob = (g * 2 + ec) % 2
                            for ki, (kk, kw) in enumerate(ktiles):
                                pb = 4 + (cnt % 4)
                                cnt += 1
                                for jc in range(2):
                                    f = (lambda jc=jc, g=g, ec=ec, kk=kk, kw=kw, pb=pb: nc.tensor.matmul(
                                        psum[:, pb, 0:kw], lhsT=fw_[:, g, jc, ec * 128:(ec + 1) * 128], rhs=fT[:, g * 2 + jc, kk:kk + kw], start=(jc == 0), stop=(jc == 1)))
                                    if jc == 1:
                                        S.op('pe', f, reads=[rfw, rfT], writes=[PB[pb]])
                                    else:
                                        S.op_nosig('pe', f, reads=[rfw, rfT], writes=[PB[pb]])
                                S.op('dve', lambda ob=ob, kk=kk, kw=kw, pb=pb: nc.vector.tensor_copy(out=yo[ob][:, kk:kk + kw], in_=psum[:, pb, 0:kw]), reads=[PB[pb]], writes=[ryo[ob]])
                            S.dma('sp', yT_d[24 + g * 2 + ec, :, k0:k0 + nk], yo[ob][:, 0:nk], reads=[ryo[ob]], writes=[reg('yT_d')])
                    S.barrier()

                run(SEQ, uf_all, reg('uf_all'), cs_in, 0, T, 1.0 / np.sqrt(SEQ * 256.0))
                if do_ctx:
                    run(NCTX, vf_d[T:NT, 2048:3072], reg('vf_d'), csc_in, T, NCTX, 1.0 / np.sqrt(NCTX * 256.0))
                S._wait('sp', rfw.writer)

        def fm_matmul(st, name, wfull, rwf, n_tiles, kchunks, rhs_fn, rhs_regs, tts, evac, nslots=3, group=None):
            wt = [st.enter_context(sbt(f"{name}_w{i}", [128, kchunks, 128], BF16)) for i in range(nslots)]
            rw = [Reg() for _ in range(nslots)]

            def loadw(i):
                S.dma('sp', wt[i % nslots][:].rearrange("p k n -> p (k n)"), wfull[i * 128:(i + 1) * 128, :], reads=[rwf], writes=[rw[i % nslots]])
            for i in range(min(nslots - 1, n_tiles)):
                loadw(i)
            for i in range(n_tiles):
                if i + nslots - 1 < n_tiles:
                    loadw(i + nslots - 1)
                ws = i % nslots
                segs = group(i) if group else [(0, kchunks, None)]
                for ti, (c0, w) in enumerate(tts):
                    for (k0, k1, tag) in segs:
                        pb = evac.alloc(i, ti, tag)
                        for kc in range(k0, k1):
                            f = (lambda kc=kc, pb=pb, ws=ws, c0=c0, w=w, k0=k0, k1=k1: nc.tensor.matmul(
                                psum[:, pb, 0:w], lhsT=wt[ws][:, kc, :], rhs=rhs_fn(kc, c0, w), start=(kc == k0), stop=(kc == k1 - 1)))
                            if kc == k1 - 1:
                                S.op('pe', f, reads=[rw[ws]] + rhs_regs, writes=[PB[pb]])
                            else:
                                S.op_nosig('pe', f, reads=[rw[ws]] + rhs_regs, writes=[PB[pb]])
                        evac.run(i, ti, tag, pb, c0, w)
                evac.done(i)

        class Evac:
            def __init__(self, nbanks=6):
                self.n = 0
                self.nb = nbanks

            def alloc(self, i, ti, tag):
                pb = self.n % self.nb
                self.n += 1
                return pb

            def run(self, i, ti, tag, pb, c0, w):
                pass

            def done(self, i):
                pass

        def token_tiles(do_ctx):
            return [(0, 512), (512, 512)] + ([(1024, 256)] if do_ctx else [])

        def stage_C(l, do_ctx, src_lat, src_ctx):
            tts = token_tiles(do_ctx)
            wcols = NT if do_ctx else T
            with contextlib.ExitStack() as st:
                hT = st.enter_context(sbt("hTc", [128, 32, NT], BF16))
                pv = st.enter_context(sbt("pvC", [128, PV_N], F32))
                rh, rpv = Reg(), Reg()
                for i in range(4):
                    S.dma('sp', hT[:, 8 * i:8 * i + 8, :], hT_d[8 * i:8 * i + 8].rearrange("f p t -> p f t"), reads=[reg('hT_d')], writes=[rh])
                S.dma('sp', pv[:], pvec_in[l], writes=[rpv])
                go = [st.enter_context(sbt(f"go{i}", [128, NT], BF16)) for i in range(2)]
                rgo = [Reg(), Reg()]

                class E(Evac):
                    def run(self, i, ti, tag, pb, c0, w):
                        ob = i % 2
                        S.op('act', lambda: nc.scalar.activation(out=go[ob][:, c0:c0 + w], in_=psum[:, pb, 0:w], func=AF.Sigmoid, bias=pv[:, PV_BG + i:PV_BG + i + 1], scale=1.0),
                             reads=[PB[pb], rpv], writes=[rgo[ob]])

                    def done(self, i):
                        ob = i % 2
                        S.dma('sp', gate_d[i, :, 0:wcols], go[ob][:, 0:wcols], reads=[rgo[ob]], writes=[reg('gate_d')])
                fm_matmul(st, "c1a", w_full['wgate'][l], reg(f'wfull_wgate{l}'), 96, 32, lambda kc, c0, w: hT[:, kc, c0:c0 + w], [rh], tts, E())
                S.barrier()
            with contextlib.ExitStack() as st:
                yT = st.enter_context(sbt("yTc", [128, 32, NT], BF16))
                ry = Reg()
                for i in range(4):
                    S.dma('sp', yT[:, 8 * i:8 * i + 8, 0:wcols], yT_d[8 * i:8 * i + 8, :, 0:wcols].rearrange("f p t -> p f t"), reads=[reg('yT_d')], writes=[ry])
                gt = [st.enter_context(sbt(f"gt{i}", [128, 3, NT], BF16)) for i in range(2)]
                rgt = [Reg(), Reg()]
                acc = [st.enter_context(sbt(f"acc{i}", [128, NT], F32)) for i in range(2)]
                racc = [Reg(), Reg()]
                tmpb = [st.enter_context(sbt(f"tmpb{i}", [128, 512], F32)) for i in range(2)]
                rtb = [Reg(), Reg()]
                mo = [st.enter_context(sbt(f"mo{i}", [128, NT], BF16)) for i in range(2)]
                rmo = [Reg(), Reg()]

                def loadg(i):
                    S.dma('sp', gt[i % 2][:, :, 0:wcols], gate_d[3 * i:3 * i + 3, :, 0:wcols].rearrange("b p t -> p b t"), reads=[reg('gate_d')], writes=[rgt[i % 2]])
                loadg(0)
                tcnt = [0]

                class E(Evac):
                    def run(self, i, ti, tag, pb, c0, w):
                        ob = i % 2
                        if tag == 0:
                            if ti == 0 and i + 1 < 32:
                                loadg(i + 1)
                            S.op('dve', lambda: nc.vector.tensor_tensor(out=acc[ob][:, c0:c0 + w], in0=psum[:, pb, 0:w], in1=gt[ob][:, 0, c0:c0 + w], op=ALU.mult),
                                 reads=[PB[pb], rgt[ob]], writes=[racc[ob]])
                        else:
                            tb = tcnt[0] % 2
                            tcnt[0] += 1
                            S.op('dve', lambda: nc.vector.tensor_tensor(out=tmpb[tb][:, 0:w], in0=psum[:, pb, 0:w], in1=gt[ob][:, tag, c0:c0 + w], op=ALU.mult),
                                 reads=[PB[pb], rgt[ob]], writes=[rtb[tb]])
                            if tag == 1:
                                S.op('dve', lambda: nc.vector.tensor_tensor(out=acc[ob][:, c0:c0 + w], in0=acc[ob][:, c0:c0 + w], in1=tmpb[tb][:, 0:w], op=ALU.add),
                                     reads=[racc[ob], rtb[tb]], writes=[racc[ob]])
                            else:
                                S.op('dve', lambda: nc.vector.tensor_tensor(out=mo[ob][:, c0:c0 + w], in0=acc[ob][:, c0:c0 + w], in1=tmpb[tb][:, 0:w], op=ALU.add),
                                     reads=[racc[ob], rtb[tb]], writes=[rmo[ob]])

                    def done(self, i):
                        ob = i % 2
                        S.dma('sp', mT_d[i, :, 0:wcols], mo[ob][:, 0:wcols], reads=[rmo[ob]], writes=[reg('mT_d')])
                fm_matmul(st, "c1b", w_full['wbr'][l], reg(f'wfull_wbr{l}'), 32, 32, lambda kc, c0, w: yT[:, kc, c0:c0 + w], [ry], tts, E(),
                          group=lambda i: [(0, 8, 0), (8, 24, 1), (24, 32, 2)])
                S.barrier()
            with contextlib.ExitStack() as st:
                AB, pv, rab, rp_ = load_mod(st, l, 0)
                mT = st.enter_context(sbt("mTc", [128, 32, NT], BF16))
                rm_ = Reg()
                for i in range(4):
                    S.dma('sp', mT[:, 8 * i:8 * i + 8, 0:wcols], mT_d[8 * i:8 * i + 8, :, 0:wcols].rearrange("f p t -> p f t"), reads=[reg('mT_d')], writes=[rm_])
                xr = [st.enter_context(sbt(f"xr{i}", [128, NT], F32)) for i in range(2)]
                rxr = [Reg(), Reg()]
                xo = [st.enter_context(sbt(f"xo{i}", [128, NT], F32)) for i in range(2)]
                rxo = [Reg(), Reg()]

                def loadx(i):
                    S.dma('sp', xr[i % 2][:, 0:T], src_lat[i], writes=[rxr[i % 2]])
                    if do_ctx:
                        S.dma('sp', xr[i % 2][:, T:NT], src_ctx[i], writes=[rxr[i % 2]])
                loadx(0)

                class E(Evac):
                    def run(self, i, ti, tag, pb, c0, w):
                        ob = i % 2
                        if ti == 0 and i + 1 < 32:
                            loadx(i + 1)
                        typ = 1 if c0 >= T else 0
                        S.op('dve', lambda: nc.vector.scalar_tensor_tensor(out=xo[ob][:, c0:c0 + w], in0=psum[:, pb, 0:w], scalar=AB[:, 4 + typ, i:i + 1], in1=xr[ob][:, c0:c0 + w],
                                                                           op0=ALU.mult, op1=ALU.add), reads=[PB[pb], rab, rxr[ob]], writes=[rxo[ob]])

                    def done(self, i):
                        ob = i % 2
                        S.dma('sp', xmT_d[i, :, 0:wcols], xo[ob][:, 0:wcols], reads=[rxo[ob]], writes=[reg('xmT_d')])
                fm_matmul(st, "c2", w_full['wo'][l], reg(f'wfull_wo{l}'), 32, 32, lambda kc, c0, w: mT[:, kc, c0:c0 + w], [rm_], tts, E())
                S.barrier()

        def exchange2():
            rx = reg('xmT_d')
            for q4 in range(4):
                S.dma('sp', xb_in[8 * q4:8 * q4 + 8, :], xmT_d[8 * q4:8 * q4 + 8, :, 0], reads=[rx], writes=[reg('xb_in')], allow_slow_non_contiguous=True)
                S.dma('sp', xb_in[32 + 8 * q4:32 + 8 * q4 + 8, :], xmT_d[8 * q4:8 * q4 + 8, :, T - 1], reads=[rx], writes=[reg('xb_in')], allow_slow_non_contiguous=True)
            S.allgather(xb_in, xb_all, reg('xb_in'), reg('xb_all'))
            S.dma('sp', xb_prev, xb_all[bass.ds(prev_c * 64 + 32, 32), :], reads=[reg('xb_all')], writes=[reg('xb_halo')])
            S.dma('sp', xb_next, xb_all[bass.ds(next_c * 64, 32), :], reads=[reg('xb_all')], writes=[reg('xb_halo')])

        def stage_D(l, do_ctx, dst_lat, dst_ctx, rdst):
            NT2 = NT + 2
            with contextlib.ExitStack() as st:
                h2 = st.enter_context(sbt("h2T", [128, 32, NT2], BF16))
                rh = Reg('h2T')
                with contextlib.ExitStack() as st2:
                    norm_stage(st2, l, 1, xmT_d[:, :, 0:T], xmT_d[:, :, T:NT], h2, rh, do_ctx, halo=(xb_all, reg('xb_halo')))
                    if not do_ctx:
                        S.op('dve', lambda: nc.vector.memset(h2[:, :, T:NT], 0.0), writes=[rh])
                    S.barrier()
                pv = st.enter_context(sbt("pvD", [128, PV_N], F32))
                rpv = Reg()
                S.dma('sp', pv[:], pvec_in[l], writes=[rpv])
                wt = [st.enter_context(sbt(f"wup{i}", [128, 8, 4, 256], BF16)) for i in range(3)]
                rw = [Reg() for _ in range(3)]
                WL = T + 2
                WCX = NCTX + 2
                Ub = [st.enter_context(sbt(f"Ub{i}", [128, WL + WCX], F32)) for i in range(2)]
                rUb = [Reg(), Reg()]
                Cg = st.enter_context(sbt("Cg", [128, NT], F32))
                Cv = st.enter_context(sbt("Cv", [128, NT], F32))
                Sg = st.enter_context(sbt("Sg", [128, NT], F32))
                rCg, rCv, rSg = Reg(), Reg(), Reg()
                ao = [st.enter_context(sbt(f"ao{i}", [128, NT], BF16)) for i in range(2)]
                rao = [Reg(), Reg()]
                for i in range(2):
                    S.op('dve', lambda i=i: nc.vector.memset(Ub[i][:], 0.0), writes=[rUb[i]])
                wfull = w_full['wup'][l]
                rwf = reg(f'wfull_wup{l}')
                wv = wfull.rearrange("(kg j p) n -> kg j p n", kg=8, j=NPAIR)

                def loadw(j):
                    S.dma('sp', wt[j % 3][:].rearrange("p kg a n -> p kg (a n)"), wv[:, j].rearrange("kg p n -> p kg n"), reads=[rwf], writes=[rw[j % 3]])
                loadw(0)
                loadw(1)
                tts = [(0, 512), (512, 512), (1024, 258)]
                wcols = NT if do_ctx else T
                for j in range(NPAIR):
                    if j + 2 < NPAIR:
                        loadw(j + 2)
                    ws = j % 3
                    for half in range(2):
                        ch = j + half * NPAIR
                        ub = Ub[half]
                        for ti, (c0, w) in enumerate(tts):
                            pb = half * 3 + ti
                            for kc in range(32):
                                f = (lambda kc=kc, pb=pb, ws=ws, c0=c0, w=w, half=half: nc.tensor.matmul(
                                    psum[:, pb, 0:w], lhsT=wt[ws][:, kc // 4, kc % 4, half * 128:(half + 1) * 128], rhs=h2[:, kc, c0:c0 + w], start=(kc == 0), stop=(kc == 31)))
                                if kc == 31:
                                    S.op('pe', f, reads=[rw[ws], rh], writes=[PB[pb]])
                                else:
                                    S.op_nosig('pe', f, reads=[rw[ws], rh], writes=[PB[pb]])
                            if ti < 2:
                                S.op('act', lambda pb=pb, ub=ub, c0=c0: nc.scalar.activation(out=ub[:, 1 + c0:1 + c0 + 512], in_=psum[:, pb, 0:512], func=AF.Identity),
                                     reads=[PB[pb]], writes=[rUb[half]])
                            else:
                                if do_ctx:
                                    S.op('act', lambda pb=pb, ub=ub: nc.scalar.activation(out=ub[:, WL + 1:WL + 1 + NCTX], in_=psum[:, pb, 0:NCTX], func=AF.Identity),
                                         reads=[PB[pb]], writes=[rUb[half]])
                                S.op('act', lambda pb=pb, ub=ub: nc.scalar.activation(out=ub[:, 0:1], in_=psum[:, pb, 256:257], func=AF.Identity), reads=[PB[pb]], writes=[rUb[half]])
                                S.op('act', lambda pb=pb, ub=ub: nc.scalar.activation(out=ub[:, WL - 1:WL], in_=psum[:, pb, 257:258], func=AF.Identity), reads=[PB[pb]], writes=[rUb[half]])
                        Co, rCo = (Cg, rCg) if half == 0 else (Cv, rCv)
                        segs = [(0, 0, T)] + ([(WL, T, NCTX)] if do_ctx else [])
                        for (o, c0, n) in segs:
                            S.op('act', lambda ub=ub, Co=Co, o=o, c0=c0, n=n, ch=ch: nc.scalar.activation(
                                out=Co[:, c0:c0 + n], in_=ub[:, o + 1:o + 1 + n], func=AF.Identity, scale=pv[:, PV_CW1 + ch:PV_CW1 + ch + 1], bias=pv[:, PV_CB + ch:PV_CB + ch + 1]),
                                reads=[rUb[half], rpv], writes=[rCo])
                            S.op('dve', lambda ub=ub, Co=Co, o=o, c0=c0, n=n, ch=ch: nc.vector.scalar_tensor_tensor(
                                out=Co[:, c0:c0 + n], in0=ub[:, o:o + n], scalar=pv[:, PV_CW0 + ch:PV_CW0 + ch + 1], in1=Co[:, c0:c0 + n], op0=ALU.mult, op1=ALU.add),
                                reads=[rUb[half], rpv, rCo], writes=[rCo])
                            S.op('dve', lambda ub=ub, Co=Co, o=o, c0=c0, n=n, ch=ch: nc.vector.scalar_tensor_tensor(
                                out=Co[:, c0:c0 + n], in0=ub[:, o + 2:o + 2 + n], scalar=pv[:, PV_CW2 + ch:PV_CW2 + ch + 1], in1=Co[:, c0:c0 + n], op0=ALU.mult, op1=ALU.add),
                                reads=[rUb[half], rpv, rCo], writes=[rCo])
                    ob = j % 2
                    S.op('act', lambda: nc.scalar.activation(out=Sg[:, 0:wcols], in_=Cg[:, 0:wcols], func=AF.Silu), reads=[rCg], writes=[rSg])
                    S.op('dve', lambda ob=ob: nc.vector.tensor_tensor(out=ao[ob][:, 0:wcols], in0=Sg[:, 0:wcols], in1=Cv[:, 0:wcols], op=ALU.mult), reads=[rSg, rCv], writes=[rao[ob]])
                    S.dma('sp', aT_d[j, :, 0:wcols], ao[ob][:, 0:wcols], reads=[rao[ob]], writes=[reg('aT_d')])
                S.barrier()
            with contextlib.ExitStack() as st:
                AB, pv, rab, rp_ = load_mod(st, l, 1)
                aT = st.enter_context(sbt("aTt", [128, NPAIR, 512], BF16))
                ra = Reg()
                wt = [st.enter_context(sbt(f"wdn{i}", [128, NPAIR, 128], BF16)) for i in range(2)]
                rw = [Reg(), Reg()]
                xr = [st.enter_context(sbt(f"xrd{i}", [128, 512], F32)) for i in range(2)]
                rxr = [Reg(), Reg()]
                xo = [st.enter_context(sbt(f"xod{i}", [128, 512], F32)) for i in range(2)]
                rxo = [Reg(), Reg()]
                wfull = w_full['wdown'][l]
                rwf = reg(f'wfull_wdown{l}')
                n = 0
                for (c0, w) in token_tiles(do_ctx):
                    typ = 1 if c0 >= T else 0
                    for (j0, j1) in ((0, 22), (22, 43), (43, 65), (65, 86)):
                        S.dma('sp', aT[:, j0:j1, 0:w], aT_d[j0:j1, :, c0:c0 + w].rearrange("j p t -> p j t"), reads=[reg('aT_d')], writes=[ra])

                    def loadw(oc, n_):
                        S.dma('sp', wt[n_ % 2][:].rearrange("p k n -> p (k n)"), wfull[oc * 128:(oc + 1) * 128, :], reads=[rwf], writes=[rw[n_ % 2]])
                        S.dma('sp', xr[n_ % 2][:, 0:w], xmT_d[oc, :, c0:c0 + w], reads=[reg('xmT_d')], writes=[rxr[n_ % 2]])
                    loadw(0, n)
                    for oc in range(32):
                        if oc + 1 < 32:
                            loadw(oc + 1, n + 1)
                        ws = n % 2
                        pb = n % 6
                        for kc in range(NPAIR):
                            f = (lambda kc=kc, pb=pb, ws=ws, w=w: nc.tensor.matmul(psum[:, pb, 0:w], lhsT=wt[ws][:, kc, :], rhs=aT[:, kc, 0:w], start=(kc == 0), stop=(kc == NPAIR - 1)))
                            if kc == NPAIR - 1:
                                S.op('pe', f, reads=[rw[ws], ra], writes=[PB[pb]])
                            else:
                                S.op_nosig('pe', f, reads=[rw[ws], ra], writes=[PB[pb]])
                        S.op('dve', lambda pb=pb, ws=ws, w=w, oc=oc, typ=typ: nc.vector.scalar_tensor_tensor(
                            out=xo[ws][:, 0:w], in0=psum[:, pb, 0:w], scalar=AB[:, 4 + typ, oc:oc + 1], in1=xr[ws][:, 0:w], op0=ALU.mult, op1=ALU.add),
                            reads=[PB[pb], rab, rxr[ws]], writes=[rxo[ws]])
                        if typ == 0:
                            S.dma('sp', dst_lat[oc, :, c0:c0 + w], xo[ws][:, 0:w], reads=[rxo[ws]], writes=[rdst])
                        else:
                            S.dma('sp', dst_ctx[oc, :, 0:w], xo[ws][:, 0:w], reads=[rxo[ws]], writes=[rdst])
                        n += 1
                S.barrier()

        stages = ['mod', 'A0', 'B0', 'C0', 'D0', 'A1', 'B1', 'C1', 'D1', 'all']
        lim = stages.index(upto)
        wnames_A = ['winA', 'winB']
        prep_weights(0, ['winA', 'winB'])
        mod_stage()
        if lim >= 1:
            prep_weights(0, ['wgate', 'wbr', 'wo', 'wup', 'wdown'])
            stage_A(0, xT_in, xcT_in)
            exchange1()
        if lim >= 2:
            stage_attn(0, True)
            stage_pool(0, True)
            stage_fnet(0, True)
        if lim >= 3:
            stage_C(0, True, xT_in, xcT_in)
            exchange2()
        if lim >= 4:
            prep_weights(1, ['winA', 'winB', 'wgate', 'wbr', 'wo', 'wup', 'wdown'])
            stage_D(0, True, x1T_d[:, :, 0:T], x1T_d[:, :, T:NT], reg('x1T_d'))
        if lim >= 5:
            stage_A(1, x1T_d[:, :, 0:T], x1T_d[:, :, T:NT])
            exchange1()
        if lim >= 6:
            stage_attn(1, False)
            stage_pool(1, False)
            stage_fnet(1, False)
        if lim >= 7:
            stage_C(1, False, x1T_d[:, :, 0:T], None)
            exchange2()
        if lim >= 8:
            stage_D(1, False, out_ap, None, reg('out'))
        last = []
        for nm in dumps:
            src, kind, sel = dump_spec[nm]
            rr = reg(dump_regs[nm])
            o = 0
            for (c0, w) in sel:
                if kind == 'fm':
                    last.append(S.dma('sp', dump_out[nm][:, :, o:o + w], src[:, :, c0:c0 + w], reads=[rr], writes=[Reg()]))
                else:
                    last.append(S.dma('sp', dump_out[nm][o:o + w, :], src[c0:c0 + w, :], reads=[rr], writes=[Reg()]))
                o += w
        S.barrier(engines=('pe', 'act', 'dve', 'sp'), include_pool=True)
        for d in last:
            S._wait('sp', d)
        if R.get('out') is not None and R['out'].writer is not None:
            S._wait('sp', R['out'].writer)
    return nc


def _tile_oc(w, c, per):
    K = w.shape[0]
    kc = K // 128
    sub = w[:, c * per * 128:(c + 1) * per * 128].reshape(kc, 128, per, 128)
    return np.ascontiguousarray(sub.transpose(2, 1, 0, 3)).reshape(per * 128, kc * 128)


def prepare_inputs(inp):
    f32 = np.float32
    x = np.asarray(inp['x'], f32)[0]
    ctx = np.asarray(inp['ctx'], f32)[0]
    c = np.asarray(inp['c'], f32)[0]
    c_ctx = np.asarray(inp['c_ctx'], f32)
    w_ada = np.asarray(inp['w_ada'], f32)
    b_ada = np.asarray(inp['b_ada'], f32)
    w_in = np.asarray(inp['w_in'], f32)
    w_gate = np.asarray(inp['w_gate'], f32)
    w_up = np.asarray(inp['w_up'], f32)
    w_down = np.asarray(inp['w_down'], f32)
    w_o = np.asarray(inp['w_o'], f32)
    rpb = np.asarray(inp['rpb'], f32)
    conv_w = np.asarray(inp['conv_w'], f32)
    conv_b = np.asarray(inp['conv_b'], f32)
    xcT = np.ascontiguousarray(ctx.T).reshape(32, 128, NCTX)
    cvec = np.stack([c.reshape(32, 128).T, c_ctx.reshape(32, 128).T], axis=-1).astype(f32)
    pvec = np.zeros((DEPTH, 128, PV_N), f32)
    for l in range(DEPTH):
        pvec[l, :, PV_N1:PV_N1 + 32] = np.asarray(inp['norm1_w'], f32)[l].reshape(32, 128).T
        pvec[l, :, PV_N2:PV_N2 + 32] = np.asarray(inp['norm2_w'], f32)[l].reshape(32, 128).T
        bg = np.asarray(inp['b_gate'], f32)[l].reshape(3, 32, 128)
        pvec[l, :, PV_BG:PV_BG + 96] = bg.transpose(2, 1, 0).reshape(128, 96)
        pvec[l, :, PV_PS:PV_PS + 8] = np.asarray(inp['pool_scale'], f32)[l].reshape(8, 128).T
        for t, off in enumerate((PV_CW0, PV_CW1, PV_CW2)):
            pvec[l, :, off:off + 172] = conv_w[l, t].reshape(172, 128).T
        pvec[l, :, PV_CB:PV_CB + 172] = conv_b[l].reshape(172, 128).T
        pvec[l, :, PV_QW] = np.asarray(inp['q_norm_w'], f32)[l]
        pvec[l, :, PV_KW] = np.asarray(inp['k_norm_w'], f32)[l]
    qcol = np.arange(64)
    cstart = np.clip(qcol - 8, 0, 48)
    kcol = np.arange(64)
    valid = (kcol[:, None] >= cstart[None, :]) & (kcol[:, None] < cstart[None, :] + 16)
    off = np.clip(kcol[:, None] - qcol[None, :] + 15, 0, 30)
    utab = np.full((DEPTH, NH, 2, 64, 9, 2, 64), NEG, f32)
    for e_idx in range(9):
        e = 2 * e_idx - 1
        for half in range(2):
            for a in range(2):
                ro = e + half - a
                if 0 <= ro <= 14:
                    tt = rpb[:, :, ro, :][:, :, off]
                    tt = np.where(valid[None, None], tt, f32(NEG))
                    utab[:, :, half, :, e_idx, a, :] = tt
    utab = utab.reshape(DEPTH, NH, 128, 9 * 128)
    cidx = np.arange(256)
    ang = 2 * np.pi * ((cidx[:, None] * cidx[None, :]) % 256) / 256.0
    cdft = np.stack([np.cos(ang), -np.sin(ang)], 0).reshape(2, 2, 128, 256).transpose(2, 0, 1, 3)
    cdft = np.ascontiguousarray(cdft).astype(ml_dtypes.bfloat16)
    sidx = np.arange(NCTX)
    angc = 2 * np.pi * ((sidx[:, None] * sidx[None, :]) % NCTX) / NCTX
    cstabc = np.stack([np.cos(angc), np.sin(angc)], 1).reshape(2, 128, 2, NCTX).astype(ml_dtypes.bfloat16)
    wbr_cat = [np.concatenate([np.asarray(inp['w_br_pool'], f32)[l], np.asarray(inp['w_br_attn'], f32)[l], np.asarray(inp['w_br_fnet'], f32)[l]], 0) for l in range(DEPTH)]
    s_all = np.arange(SEQ, dtype=np.int64)
    in_maps = []
    for cid in range(NCORES):
        m = {}
        m['xT'] = np.ascontiguousarray(x[cid * T:(cid + 1) * T].T).reshape(32, 128, T)
        m['xcT'] = xcT
        m['cvec'] = cvec
        wa = np.empty((DEPTH, 24, 128, 32, 128), f32)
        ba = np.empty((DEPTH, 128, 24, 2), f32)
        for l in range(DEPTH):
            sub = w_ada[l][:, cid * 3072:(cid + 1) * 3072].reshape(32, 128, 24, 128)
            wa[l] = sub.transpose(2, 1, 0, 3)
            bb = b_ada[l, cid * 3072:(cid + 1) * 3072].reshape(24, 128).T
            ba[l] = np.stack([bb, bb], -1)
        m['wada'] = wa
        m['bada'] = ba
        m['pvec'] = pvec
        fl = np.zeros((128, 2), f32)
        fl[:, 0] = 0.0 if cid == 0 else 1.0
        fl[:, 1] = 0.0 if cid == NCORES - 1 else 1.0
        m['flags'] = fl
        ic = np.zeros((4, NT), f32)
        for g, w in enumerate(POOL_WINDOWS):
            t = np.arange(cid * T, (cid + 1) * T)
            lo = np.clip(t - w // 2, 0, SEQ - 1)
            hi = np.clip(t + (w - 1 - w // 2), 0, SEQ - 1)
            ic[g, 0:T] = 1.0 / (hi - lo + 1)
            t = np.arange(NCTX)
            lo = np.clip(t - w // 2, 0, NCTX - 1)
            hi = np.clip(t + (w - 1 - w // 2), 0, NCTX - 1)
            ic[g, T:NT] = 1.0 / (hi - lo + 1)
        m['invcnt'] = np.ascontiguousarray(np.broadcast_to(ic[None], (128, 4, NT)))
        m['poolw'] = np.asarray(inp['pool_w'], f32)
        m['fnetw'] = np.asarray(inp['fnet_w'], f32)
        m['utab'] = utab
        rm = np.full((2, 64, 8, 7, 2, 64), NEG, f32)
        for p in range(8):
            rs_ = 2 * p if p <= 5 else 10
            for mm in range(7):
                for half in range(2):
                    Kr = 16 * cid - 4 + rs_ + 2 * mm + half
                    for a in range(2):
                        Rq = 16 * cid + 2 * p + a
                        r0 = min(max(Rq - 4, 0), 120)
                        if 0 <= Kr < 128 and r0 <= Kr < r0 + 8:
                            rm[half, :, p, mm, a, :] = 0.0
        m['rmask'] = rm.reshape(128, 8, 7 * 128)
        kk = np.arange(cid * T, (cid + 1) * T, dtype=np.int64)
        ang = 2 * np.pi * ((s_all[:, None] * kk[None, :]) % SEQ).astype(np.float64) / SEQ
        cs = np.empty((64, 128, 2, T), ml_dtypes.bfloat16)
        cs[:, :, 0, :] = np.cos(ang).reshape(64, 128, T).astype(ml_dtypes.bfloat16)
        cs[:, :, 1, :] = np.sin(ang).reshape(64, 128, T).astype(ml_dtypes.bfloat16)
        m['cstab'] = cs
        m['cstabc'] = cstabc
        m['cdft'] = cdft
        m['winA'] = np.stack([_tile_oc(w_in[l][:, 0:5120], cid, 5) for l in range(DEPTH)])
        m['winB'] = np.stack([np.ascontiguousarray(w_in[l][:, 5120 + 384 * cid:5120 + 384 * (cid + 1)].reshape(32, 128, 384).transpose(1, 0, 2)).reshape(128, 32 * 384)
                              for l in range(DEPTH)])
        wg = []
        for l in range(DEPTH):
            sub = w_gate[l].reshape(32, 128, 3, 32, 128)[:, :, :, 4 * cid:4 * cid + 4, :]
            wg.append(np.ascontiguousarray(sub.transpose(3, 2, 1, 0, 4)).reshape(12 * 128, 32 * 128))
        m['wgate'] = np.stack(wg)
        m['wbr'] = np.stack([_tile_oc(wbr_cat[l], cid, 4) for l in range(DEPTH)])
        m['wo'] = np.stack([_tile_oc(w_o[l], cid, 4) for l in range(DEPTH)])
        wu = []
        for l in range(DEPTH):
            sub = w_up[l][512 * cid:512 * (cid + 1)].reshape(4, 128, 2, NPAIR, 128)
            wu.append(np.ascontiguousarray(sub.transpose(3, 1, 0, 2, 4)).reshape(NPAIR * 128, 4 * 256))
        m['wup'] = np.stack(wu)
        m['wdown'] = np.stack([_tile_oc(w_down[l], cid, 4) for l in range(DEPTH)])
        in_maps.append(m)
    return in_maps


_CACHE = {}


def kernel(**inputs):
    in_maps = prepare_inputs(inputs)
    if 'nc' not in _CACHE:
        _CACHE['nc'] = build_program('all')
    nc = _CACHE['nc']
    res = run_bass_kernel_spmd(nc, in_maps, core_ids=list(range(NCORES)))
    outs = [np.asarray(r['outT']).reshape(D, T).T for r in res.results]
    return np.concatenate(outs, axis=0)[None].astype(np.float32)
```
